# Optimizing a Trainium2 kernel written in Bass

```python
import jax, jax.numpy as jnp
from jax import lax
import numpy as np

D_MODEL = 1024
BATCH = 4
SEQ = 8192
DEPTH = 2

RW_HEADS = 8
RW_HEAD_DIM = 64
RW_WIDTH = RW_HEADS * RW_HEAD_DIM
RW_DECAY_RANK = 64
RW_ICLR_RANK = 64
RW_GATE_RANK = 128
RW_COLS = 3 * RW_WIDTH + RW_DECAY_RANK + RW_ICLR_RANK + RW_GATE_RANK
RW_GN_EPS = 64e-5
RET_HEADS = 4
RET_QK_DIM = 64
RET_V_DIM = 128
RET_QK_WIDTH = RET_HEADS * RET_QK_DIM
RET_V_WIDTH = RET_HEADS * RET_V_DIM
RET_COLS = 2 * RET_QK_WIDTH + 2 * RET_V_WIDTH
RET_CHUNK = 128
ROPE_BASE = 10000.0
SG_GROUPS = 4
SG_CHUNK = 128
SG_WIDTH = 512
SG_GROUP_DIM = SG_WIDTH // SG_GROUPS
SG_COLS = 2 * SG_WIDTH
N_BRANCH = 3
BRANCH_WIDTH = 512
GATE_COLS = N_BRANCH * D_MODEL
IN_COLS = RW_COLS + RET_COLS + SG_COLS + GATE_COLS
D_FF = 2816
CONV_WIDTH = 3
EPS = 1e-6

kernel_name = "hybrid_rwkv7_retention_sgu_gated_merge"


def rmsnorm(x, g):
    xf = x.astype(jnp.float32)
    y = xf * lax.rsqrt(jnp.mean(xf * xf, axis=-1, keepdims=True) + EPS) * g.astype(jnp.float32)
    return y.astype(x.dtype)


def rwkv7_scan(r, w, k, v, kk, a):
    B, S, H, N = r.shape

    def step(state, inp):
        r_t, w_t, k_t, v_t, kk_t, a_t = inp
        sa = jnp.einsum('bhvk,bhk->bhv', state, -kk_t)
        state = (state * w_t[:, :, None, :]
                 + sa[..., None] * (kk_t * a_t)[:, :, None, :]
                 + v_t[..., None] * k_t[:, :, None, :])
        y = jnp.einsum('bhvk,bhk->bhv', state, r_t)
        return state, y

    xs = tuple(jnp.moveaxis(t.astype(jnp.float32), 1, 0) for t in (r, w, k, v, kk, a))
    s0 = jnp.zeros((B, H, N, N), jnp.float32)
    _, ys = lax.scan(step, s0, xs)
    return jnp.moveaxis(ys, 0, 1)


def rwkv7_branch(p, mu, w0, w_up, a0, a_up, g_up, k_k, k_a, r_k, lnx_g, lnx_b):
    B, S, _ = p.shape
    p_prev = jnp.pad(p, ((0, 0), (1, 0), (0, 0)))[:, :S]
    p = p + mu * (p_prev - p)
    cuts = [RW_WIDTH, 2 * RW_WIDTH, 3 * RW_WIDTH, 3 * RW_WIDTH + RW_DECAY_RANK,
            3 * RW_WIDTH + RW_DECAY_RANK + RW_ICLR_RANK]
    r, k, v, wd, ad, gd = jnp.split(p, cuts, axis=-1)
    w = -jax.nn.softplus(-(w0 + jnp.tanh(wd) @ w_up)) - 0.5
    decay = jnp.exp(-jnp.exp(w.astype(jnp.float32)))
    a = jax.nn.sigmoid(a0 + ad @ a_up)
    g = jax.nn.sigmoid(gd) @ g_up
    kk = (k * k_k).astype(jnp.float32)
    k = k * (1.0 + (a - 1.0) * k_a)

    def heads(t):
        return t.reshape(B, S, RW_HEADS, RW_HEAD_DIM)

    r, k, v, decay, a, kk = map(heads, (r, k, v, decay, a, kk))
    kk = kk / jnp.maximum(jnp.sqrt(jnp.sum(kk * kk, axis=-1, keepdims=True)), 1e-12)
    y = rwkv7_scan(r, decay, k, v, kk, a)
    mean = jnp.mean(y, axis=-1, keepdims=True)
    var = jnp.mean(jnp.square(y - mean), axis=-1, keepdims=True)
    y = (y - mean) * lax.rsqrt(var + RW_GN_EPS) * lnx_g + lnx_b
    y = y + jnp.sum(r * k * r_k, axis=-1, keepdims=True) * v
    return (y.reshape(B, S, RW_WIDTH) * g).astype(p.dtype)


def rotary(t, cos, sin):
    t1, t2 = jnp.split(t, 2, axis=-1)
    c = cos[None, :, None, :]
    s = sin[None, :, None, :]
    return jnp.concatenate([t1 * c - t2 * s, t2 * c + t1 * s], axis=-1)


def retention_branch(p, cos, sin):
    B, S, _ = p.shape
    nC = S // RET_CHUNK
    q, k, v, g = jnp.split(p, [RET_QK_WIDTH, 2 * RET_QK_WIDTH, 2 * RET_QK_WIDTH + RET_V_WIDTH], axis=-1)
    q = rotary(q.reshape(B, S, RET_HEADS, RET_QK_DIM), cos, sin)
    k = rotary(k.reshape(B, S, RET_HEADS, RET_QK_DIM), cos, sin) * (RET_QK_DIM ** -0.5)
    v = v.reshape(B, S, RET_HEADS, RET_V_DIM)

    def chunks(t):
        return t.reshape(B, nC, RET_CHUNK, RET_HEADS, t.shape[-1]).transpose(0, 3, 1, 2, 4)

    qc, kc, vc = chunks(q), chunks(k), chunks(v)
    log_gamma = jnp.log(1.0 - 2.0 ** (-5.0 - jnp.arange(RET_HEADS, dtype=jnp.float32)))
    pos = jnp.arange(RET_CHUNK, dtype=jnp.float32)
    diff = pos[:, None] - pos[None, :]
    decay_in = jnp.where(diff[None] >= 0,
                         jnp.exp(jnp.maximum(diff, 0.0)[None] * log_gamma[:, None, None]), 0.0)
    scores = jnp.einsum('bhnid,bhnjd->bhnij', qc, kc) * decay_in[:, None]
    inner = jnp.einsum('bhnij,bhnje->bhnie', scores, vc)
    k_dec = jnp.exp((RET_CHUNK - 1.0 - pos)[None] * log_gamma[:, None])
    kv = jnp.einsum('bhnjd,bhnje,hj->nbhde', kc, vc, k_dec).astype(jnp.float32)
    chunk_decay = jnp.exp(RET_CHUNK * log_gamma)[None, :, None, None]

    def step(R, kv_n):
        return R * chunk_decay + kv_n, R

    _, R_prev = lax.scan(step, jnp.zeros((B, RET_HEADS, RET_QK_DIM, RET_V_DIM), jnp.float32), kv)
    q_dec = jnp.exp((pos + 1.0)[None] * log_gamma[:, None])
    cross = jnp.einsum('bhnid,nbhde,hi->bhnie', qc, R_prev, q_dec)
    y = (inner + cross).transpose(0, 2, 3, 1, 4).reshape(B, S, RET_HEADS, RET_V_DIM)
    y = y * lax.rsqrt(jnp.mean(y * y, axis=-1, keepdims=True) + EPS)
    return (y.reshape(B, S, RET_V_WIDTH) * jax.nn.silu(g)).astype(p.dtype)


def sgu_branch(p, ln_g, ln_b, w_s, b_s):
    B, S, _ = p.shape
    nC = S // SG_CHUNK
    z = jax.nn.gelu(p, approximate=False)
    u, v = jnp.split(z, 2, axis=-1)
    vf = v.astype(jnp.float32)
    mean = jnp.mean(vf, axis=-1, keepdims=True)
    var = jnp.mean(jnp.square(vf - mean), axis=-1, keepdims=True)
    v = ((vf - mean) * lax.rsqrt(var + EPS) * ln_g + ln_b).astype(p.dtype)
    vc = v.reshape(B, nC, SG_CHUNK, SG_GROUPS, SG_GROUP_DIM)
    causal = jnp.tril(jnp.ones((SG_CHUNK, SG_CHUNK), dtype=bool))
    w = jnp.where(causal[None], w_s, 0.0)
    mixed = jnp.einsum('gij,bnjgd->bnigd', w, vc) + b_s.T[:, :, None]
    return u * mixed.reshape(B, S, SG_WIDTH)


def conv_ffn(h, w_up, conv_w, conv_b, w_down):
    S = h.shape[1]
    u = h @ w_up
    up = jnp.pad(u, ((0, 0), (CONV_WIDTH - 1, 0), (0, 0)))
    c = conv_b
    for j in range(CONV_WIDTH):
        c = c + conv_w[j] * up[:, j:j + S]
    gate, val = jnp.split(c, 2, axis=-1)
    return (jax.nn.silu(gate) * val) @ w_down


def _normal(k, shape, scale):
    return scale * jax.random.normal(k, shape, jnp.float32)


def setup_inputs(seed: int = 0) -> dict:
    key = jax.random.key(seed)
    ks = jax.random.split(key, 32)
    L = DEPTH
    f32 = jnp.float32
    return {
        "x": jax.random.normal(ks[0], (BATCH, SEQ, D_MODEL), f32),
        "norm1_g": 1.0 + _normal(ks[1], (L, D_MODEL), 0.05),
        "w_in": _normal(ks[2], (L, D_MODEL, IN_COLS), D_MODEL ** -0.5),
        "rw_mu": jax.random.uniform(ks[3], (L, RW_COLS), f32),
        "rw_w0": jax.random.uniform(ks[4], (L, RW_WIDTH), f32, -6.0, 1.0),
        "rw_w_up": _normal(ks[5], (L, RW_DECAY_RANK, RW_WIDTH), 0.1 * RW_DECAY_RANK ** -0.5),
        "rw_a0": _normal(ks[6], (L, RW_WIDTH), 0.1),
        "rw_a_up": _normal(ks[7], (L, RW_ICLR_RANK, RW_WIDTH), 0.3 * RW_ICLR_RANK ** -0.5),
        "rw_g_up": _normal(ks[8], (L, RW_GATE_RANK, RW_WIDTH), RW_GATE_RANK ** -0.5),
        "rw_k_k": 0.85 + _normal(ks[9], (L, RW_WIDTH), 0.05),
        "rw_k_a": 1.0 + _normal(ks[10], (L, RW_WIDTH), 0.05),
        "rw_r_k": _normal(ks[11], (L, RW_HEADS, RW_HEAD_DIM), 0.1),
        "rw_lnx_g": 1.0 + _normal(ks[12], (L, RW_HEADS, RW_HEAD_DIM), 0.05),
        "rw_lnx_b": _normal(ks[13], (L, RW_HEADS, RW_HEAD_DIM), 0.02),
        "sg_ln_g": 1.0 + _normal(ks[14], (L, SG_WIDTH), 0.05),
        "sg_ln_b": _normal(ks[15], (L, SG_WIDTH), 0.02),
        "sg_w_s": _normal(ks[16], (L, SG_GROUPS, SG_CHUNK, SG_CHUNK), SG_CHUNK ** -0.5),
        "sg_b": 1.0 + _normal(ks[17], (L, SG_GROUPS, SG_CHUNK), 0.1),
        "w_branch": _normal(ks[18], (L, N_BRANCH, BRANCH_WIDTH, D_MODEL), BRANCH_WIDTH ** -0.5),
        "w_out": _normal(ks[19], (L, D_MODEL, D_MODEL), 0.5 * D_MODEL ** -0.5),
        "norm2_g": 1.0 + _normal(ks[20], (L, D_MODEL), 0.05),
        "ffn_w_up": _normal(ks[21], (L, D_MODEL, 2 * D_FF), D_MODEL ** -0.5),
        "ffn_conv_w": _normal(ks[22], (L, CONV_WIDTH, 2 * D_FF), CONV_WIDTH ** -0.5),
        "ffn_conv_b": _normal(ks[23], (L, 2 * D_FF), 0.02),
        "ffn_w_down": _normal(ks[24], (L, D_FF, D_MODEL), 0.5 * D_FF ** -0.5),
        "final_g": 1.0 + _normal(ks[25], (D_MODEL,), 0.05),
    }


def reference(x, norm1_g, w_in, rw_mu, rw_w0, rw_w_up, rw_a0, rw_a_up, rw_g_up, rw_k_k, rw_k_a,
              rw_r_k, rw_lnx_g, rw_lnx_b, sg_ln_g, sg_ln_b, sg_w_s, sg_b, w_branch, w_out,
              norm2_g, ffn_w_up, ffn_conv_w, ffn_conv_b, ffn_w_down, final_g):
    B, S, _ = x.shape
    inv_freq = 1.0 / (ROPE_BASE ** jnp.linspace(0.0, 1.0, RET_QK_DIM // 2, dtype=jnp.float32))
    ang = jnp.arange(S, dtype=jnp.float32)[:, None] * inv_freq[None, :]
    cos, sin = jnp.cos(ang), jnp.sin(ang)
    in_cuts = [RW_COLS, RW_COLS + RET_COLS, RW_COLS + RET_COLS + SG_COLS]
    for l in range(DEPTH):
        h = rmsnorm(x, norm1_g[l])
        p = h @ w_in[l]
        p_rw, p_ret, p_sg, gate_logits = jnp.split(p, in_cuts, axis=-1)
        y_rw = rwkv7_branch(p_rw, rw_mu[l], rw_w0[l], rw_w_up[l], rw_a0[l], rw_a_up[l], rw_g_up[l],
                            rw_k_k[l], rw_k_a[l], rw_r_k[l], rw_lnx_g[l], rw_lnx_b[l])
        y_ret = retention_branch(p_ret, cos, sin)
        y_sg = sgu_branch(p_sg, sg_ln_g[l], sg_ln_b[l], sg_w_s[l], sg_b[l])
        branches = jnp.stack([y_rw, y_ret, y_sg], axis=2)
        proj = jnp.einsum('bsgc,gcd->bsgd', branches, w_branch[l])
        gates = jax.nn.sigmoid(gate_logits.reshape(B, S, N_BRANCH, D_MODEL))
        x = x + jnp.sum(gates * proj, axis=2) @ w_out[l]
        x = x + conv_ffn(rmsnorm(x, norm2_g[l]), ffn_w_up[l], ffn_conv_w[l], ffn_conv_b[l], ffn_w_down[l])
    return rmsnorm(x, final_g)
```

```python
import math
import os
import numpy as np
import ml_dtypes
import concourse.bass as bass
import concourse.mybir as mybir
from concourse.bass_utils import run_bass_kernel_spmd

F32 = mybir.dt.float32
BF16 = mybir.dt.bfloat16
AF = mybir.ActivationFunctionType
ALU = mybir.AluOpType

D = 1024
SEQ = 8192
DEPTH = 2
NTOK = 256
NSUB = NTOK // 128
NST_FULL = SEQ // NTOK
DFF = 2816
NWT = 37
T_RW, T_RETQK, T_RETV, T_RETG, T_SGU, T_SGV, T_GATE, T_BR, T_OUT, T_UP, T_DN = 0, 4, 5, 6, 7, 8, 9, 15, 18, 20, 31
EPS = 1e-6
GN_EPS = 64e-5
C0 = math.exp(-0.5)

CP_N1, CP_N2, CP_MU, CP_W0, CP_A0, CP_KK, CP_KA, CP_RK, CP_LG, CP_LB = 0, 8, 16, 30, 34, 38, 42, 46, 50, 54
CP_CW0, CP_CW1, CP_CW2, CP_CB = 58, 102, 146, 190
NCP = 234


class Buf:
    def __init__(self, name, t):
        self.name = name
        self.t = t
        self.w = {}
        self.r = {}

        self.a = t.ap()

    def __getitem__(self, k):
        return self.a[k]


class View:
    def __init__(self, base, ap):
        self.base, self.a, self.name = base, ap, base.name

    def __getitem__(self, k):
        return self.a[k]


class Eng:
    def __init__(self, name, obj, sem, inc):
        self.name, self.obj, self.sem, self.inc = name, obj, sem, inc
        self.cnt = 0
        self.seen = {}


class KB:
    def __init__(self, nc):
        self.nc = nc
        self.pe = Eng("pe", nc.tensor, nc.alloc_semaphore("s_pe"), 1)
        self.act = Eng("act", nc.scalar, nc.alloc_semaphore("s_act"), 1)
        self.dve = Eng("dve", nc.vector, nc.alloc_semaphore("s_dve"), 1)
        self.pool = Eng("pool", nc.gpsimd, nc.alloc_semaphore("s_pool"), 1)
        self.sp = Eng("sp", nc.sync, None, 16)
        self.gq = Eng("gq", nc.gpsimd, None, 16)
        self.dsems = {}
        self.psum = []
        self.psi = 0
        self.pe_pending = False

    def sb(self, name, shape, dt):
        return Buf(name, self.nc.alloc_sbuf_tensor(name, list(shape), dt))

    def dram(self, name, shape, dt, kind="Internal"):
        return Buf(name, self.nc.dram_tensor(name, list(shape), dt, kind=kind))

    def dsem(self, key):
        if key not in self.dsems:
            self.dsems[key] = [self.nc.alloc_semaphore("d_" + key), 0]
        return self.dsems[key]

    def _wait(self, eng, need):
        waiter = self.pool if eng is self.gq else eng
        for sid, (sem, v) in need.items():
            if eng is self.pe and sem is self.pe.sem:
                continue
            if waiter.seen.get(sid, 0) < v:
                waiter.obj.wait_ge(sem, v)
                waiter.seen[sid] = v

    def _deps(self, eng, reads, writes):
        need = {}
        reads = [getattr(b, "base", b) for b in reads]
        writes = [getattr(b, "base", b) for b in writes]

        def add(d):
            for sid, (sem, v) in d.items():
                if sid not in need or need[sid][1] < v:
                    need[sid] = (sem, v)
        for b in reads:
            add(b.w)
        for b in writes:
            add(b.w)
            add(b.r)
        self._wait(eng, need)

    def _record(self, sig, reads, writes):
        sid = id(sig[0])
        reads = [getattr(b, "base", b) for b in reads]
        writes = [getattr(b, "base", b) for b in writes]
        for b in writes:
            if b.r:
                b.w = {}
                b.r = {}
            b.w[sid] = sig
        for b in reads:
            b.r[sid] = sig

    def op(self, eng, fn, reads=(), writes=(), signal=True):
        self._deps(eng, reads, writes)
        ins = fn(eng.obj)
        if eng is self.pe and not signal:
            sig = (eng.sem, eng.cnt + 1)
            self.pe_pending = True
        else:
            eng.cnt += 1
            ins.then_inc(eng.sem, 1)
            sig = (eng.sem, eng.cnt)
            if eng is self.pe:
                self.pe_pending = False
        self._record(sig, reads, writes)
        return ins

    def dma(self, q, out_ap, in_ap, reads=(), writes=(), key=None):
        self._deps(q, reads, writes)
        ds = self.dsem(key or writes[0].name)
        ins = q.obj.dma_start(out=out_ap, in_=in_ap)
        ds[1] += 16
        ins.then_inc(ds[0], 16)
        self._record((ds[0], ds[1]), reads, writes)
        return ins

    def ps(self):
        b = self.psum[self.psi % len(self.psum)]
        self.psi += 1
        return b


def build_program(nst, depth, stages, debug_taps=False):
    nc = bass.Bass("TRN2", target_bir_lowering=False)
    K = KB(nc)
    ntok_run = nst * NTOK
    x_in = K.dram("x", [SEQ, D], F32, kind="ExternalInput")
    wbig = K.dram("wbig", [DEPTH, NWT, 128, 4096], F32, kind="ExternalInput")
    colp_d = K.dram("colp", [DEPTH, 128, NCP], F32, kind="ExternalInput")
    rowp_d = K.dram("rowp", [DEPTH, 128, 1536], F32, kind="ExternalInput")
    fing_d = K.dram("fing", [128, D], F32, kind="ExternalInput")
    lowr_d = K.dram("lowr", [DEPTH, 128, 1024], F32, kind="ExternalInput")
    sgw_d = K.dram("sgw", [DEPTH, 128, 512], F32, kind="ExternalInput")
    cst_d = K.dram("cst", [128, 4096], F32, kind="ExternalInput")
    rot_d = K.dram("rot", [SEQ, 512], F32, kind="ExternalInput")
    y_out = K.dram("y", [SEQ, D], F32, kind="ExternalOutput")
    DBG = int(os.environ.get("DBG", "-1"))
    if DBG >= 0:
        dbg_d = K.dram("dbg", [128, 4 * NTOK], F32, kind="ExternalOutput")
    wbf = K.dram("wbf", [DEPTH, NWT, 128, 4096], BF16)
    xmid = K.dram("xmid", [SEQ, D], F32)

    ring = [K.sb(f"ring{i}", [128, 4096], BF16) for i in range(3)]
    xt = K.sb("xt", [128, NSUB, D], F32)
    xn = K.sb("xn", [128, NSUB, D], BF16)
    hT = K.sb("hT", [128, 8, NTOK], BF16)
    ssq = K.sb("ssq", [128, 8], F32)
    colp = [K.sb(f"colp{l}", [128, NCP], F32) for l in range(DEPTH)]
    fing = K.sb("fing_s", [128, D], F32)
    cst = K.sb("cst_s", [128, 3392], F32)
    ident = K.sb("ident", [128, 128], BF16)
    mrg = K.sb("mrg", [128, 8, NTOK], BF16)
    arenaA = K.sb("arenaA", [128, 22 * NTOK], BF16)
    hidden = View(arenaA, arenaA.a.rearrange("p (k t) -> p k t", k=22))
    cbuf = [K.sb(f"cbuf{i}", [128, NTOK + 2], F32) for i in range(2)]
    acc = [K.sb(f"acc{i}", [128, NTOK], F32) for i in range(3)]
    gsil = [K.sb(f"gsil{i}", [128, NTOK], F32) for i in range(2)]
    chalo = K.sb("chalo", [128, 44, 2], F32)
    ybr = [K.sb(f"ybr{g}", [128, 4, NTOK], BF16) for g in range(3)]
    gts = [K.sb(f"gts{i}", [128, NTOK], BF16) for i in range(3)]
    macc = [K.sb(f"macc{i}", [128, NTOK], F32) for i in range(2)]
    NT_ = NTOK
    rowp = K.sb("rowp_s", [128, 1536], F32)
    lowr32 = View(xt, xt.a[:, 0, :])
    lowr = K.sb("lowr_b", [128, 1024], BF16)
    sgw32 = View(xt, xt.a[:, 1, 0:512])
    wsT = K.sb("wsT", [128, 4, 128], BF16)
    onesb = K.sb("onesb", [128, 128], BF16)
    blkb = K.sb("blkb", [128, 128], BF16)
    omka = K.sb("omka", [128, 4], F32)
    uT = K.sb("uT", [128, 4, NT_], BF16)
    vg = [K.sb(f"vg{i}", [128, 512], F32) for i in range(1)] * 2
    vnb = [K.sb(f"vnb{i}", [128, 512], BF16) for i in range(1)] * 2
    st8 = K.sb("st8", [128, 16], F32)
    t512 = [K.sb(f"t512_{i}", [128, 512], F32) for i in range(3)]
    rot = K.sb("rot_s", [128, 512], F32)
    qkraw = K.sb("qkraw", [128, 512], F32)
    rtmp = [View(t512[i // 2], t512[i // 2].a[:, (i % 2) * 256:(i % 2 + 1) * 256]) for i in range(4)]
    qkr = K.sb("qkr", [128, 512], BF16)
    kd = K.sb("kd", [128, NSUB, 256], BF16)
    qkT = K.sb("qkT", [128, NSUB, 4, 128], BF16)
    qdT = K.sb("qdT", [128, NSUB, 2, 128], BF16)
    vtm = K.sb("vtm", [128, NSUB, 512], BF16)
    gT = K.sb("gT", [128, 4, NT_], BF16)
    smT = K.sb("smT", [128, 4, 128], BF16)
    sqb = K.sb("sqb", [128, 512], BF16)
    R32 = K.sb("R32", [128, 2, 128], F32)
    Rb = K.sb("Rb", [128, 4, 128], BF16)
    NCH = NT_ // 64
    praw = [K.sb(f"praw{i}", [128, NT_ + 1], F32) for i in range(2)]
    phalo = K.sb("phalo", [128, 14], F32)
    rF = K.sb("rF", [128, 4, NT_], F32)
    kF = K.sb("kF", [128, 4, NT_], F32)
    vF = K.sb("vF", [128, 4, NT_], F32)
    lrb = K.sb("lrb", [128, 2, NT_], BF16)
    gF = K.sb("gF", [128, 4, NT_], BF16)
    bonus = K.sb("bonus", [128, 4, NT_], BF16)
    AR = K.sb("AR", [128, 4, NCH, 2, 64], BF16)
    BK = K.sb("BK", [128, 4, NCH, 2, 64], BF16)
    VV = K.sb("VV", [128, 4, NCH, 2, 64], BF16)
    gam = K.sb("gam", [128, 4, NT_], F32)
    rt = [K.sb(f"rt{i}", [128, NT_], F32) for i in range(10)]
    rtb = [K.sb(f"rtb{i}", [128, NT_], BF16) for i in range(2)]
    G = View(arenaA, arenaA.a[:, 0:4096].rearrange("p (c h m) -> p c h m", c=NCH, h=8))
    Nn = [K.sb(f"Nn{i}", [64, NCH, 8, 64], BF16) for i in range(2)]
    NTn = [K.sb(f"NTn{i}", [64, NCH, 8, 64], BF16) for i in range(2)]
    Pp = [K.sb(f"Pp{i}", [64, NCH, 8, 64], BF16) for i in range(2)]
    BKT = View(mrg, mrg.a.rearrange("p k t -> p (k t)").rearrange("p (c a m) -> p c a m", c=NCH, a=4))
    VT = View(xn, xn.a.rearrange("p s d -> p (s d)").rearrange("p (c a m) -> p c a m", c=NCH, a=4))
    XT = View(arenaA, arenaA.a[0:64, 4096:4608].rearrange("p (h v) -> p h v", h=8))
    UT = View(arenaA, arenaA.a[0:64, 4608:5120].rearrange("p (h v) -> p h v", h=8))
    ST32 = K.sb("ST32", [128, 4, 64], F32)
    STb = K.sb("STb", [128, 8, 64], BF16)
    sttmp = K.sb("sttmp", [128, 4, 64], F32)
    Yf = K.sb("Yf", [128, 4, NT_], F32)
    K.psum = [Buf(f"ps{i}", nc.alloc_psum_tensor(f"ps{i}", [128, 512], F32)) for i in range(8)]

    sp, gq, pe, act, dve, pool = K.sp, K.gq, K.pe, K.act, K.dve, K.pool

    K.dma(sp, cst[:, :], cst_d[:, 0:3392], writes=[cst])
    K.dma(sp, fing[:, :], fing_d[:, :], writes=[fing])
    for l in range(depth):
        K.dma(sp, colp[l][:, :], colp_d[l, :, :], writes=[colp[l]])
    K.op(dve, lambda e: e.tensor_copy(ident[:, :], cst[:, 0:128]), reads=[cst], writes=[ident])
    for l in range(depth):
        for t in range(NWT):
            if t < T_BR and not (stages & {"rw", "ret", "sgu"}) :
                pass
            K.dma(gq, wbf[l, t, :, :], wbig[l, t, :, :], writes=[wbf], key="wbf")

    ring_i = [0]

    def mm(out, lhsT, rhs, reads, writes, start=True, stop=True, signal=False):
        K.op(pe, lambda e: e.matmul(out, lhsT, rhs, start=start, stop=stop), reads=reads, writes=writes, signal=signal)

    def wload(l, t):
        b = ring[ring_i[0] % len(ring)]
        ring_i[0] += 1
        K.dma(sp, b[:, :], wbf[l, t, :, :], reads=[wbf], writes=[b])
        return b

    def rstd_from(sq, n, scale, eps):
        K.op(dve, lambda e: e.tensor_scalar(sq[:, 4:4 + n], sq[:, 0:n], scale, eps, ALU.mult, ALU.add), reads=[sq], writes=[sq])
        K.op(act, lambda e: e.activation(out=sq[:, 4:4 + n], in_=sq[:, 4:4 + n], func=AF.Sqrt), reads=[sq], writes=[sq])
        K.op(dve, lambda e: e.reciprocal(sq[:, 4:4 + n], sq[:, 4:4 + n]), reads=[sq], writes=[sq])

    def rmsnorm_T(l, src, gcol):
        K.op(pool, lambda e: e.memset(ssq[:, 0:NSUB], 0.0), writes=[ssq])
        for s in range(NSUB):
            K.op(act, lambda e, s=s: e.activation(out=xn[:, s, :], in_=src[:, s, :], func=AF.Square,
                                                  accum_out=ssq[:, s:s + 1]),
                 reads=[src], writes=[xn, ssq])
        rstd_from(ssq, NSUB, 1.0 / D, EPS)
        for s in range(NSUB):
            K.op(dve, lambda e, s=s: e.tensor_scalar(xn[:, s, :], src[:, s, :], ssq[:, 4 + s:5 + s], None, ALU.mult),
                 reads=[src, ssq], writes=[xn])
        for kc in range(8):
            p = K.ps()
            pv = p[:, :].bitcast(BF16)
            for s in range(NSUB):
                K.op(pe, lambda e, s=s, kc=kc, pv=pv: e.transpose(pv[:, s * 128:(s + 1) * 128],
                                                                   xn[:, s, kc * 128:(kc + 1) * 128], ident[:, :]),
                     reads=[xn, ident], writes=[p], signal=(s == NSUB - 1))
            K.op(act, lambda e, kc=kc, pv=pv: e.activation(out=hT[:, kc, :], in_=pv[:, 0:NTOK], func=AF.Copy,
                                                            scale=colp[l][:, gcol + kc:gcol + kc + 1]),
                 reads=[p, colp[l]], writes=[hT])

    def ffn(l, st, last):
        cp = colp[l]
        rmsnorm_T(l, xt, CP_N2)
        for t in range(11):
            wb = wload(l, T_UP + t)
            w3 = wb[:, :].rearrange("p (k c) -> p k c", k=8)
            for q in range(4):
                idx = 4 * t + q
                p = K.ps()
                for kc in range(8):
                    K.op(pe, lambda e, kc=kc, q=q, p=p, w3=w3: e.matmul(p[:, 0:NTOK], w3[:, kc, q * 128:(q + 1) * 128],
                                                                      hT[:, kc, :], start=(kc == 0), stop=(kc == 7)),
                         reads=[wb, hT], writes=[p], signal=(kc == 7))
                cb = cbuf[idx % 2]
                ac = acc[idx % 3]
                K.op(pool, lambda e, cb=cb, idx=idx: e.tensor_copy(cb[:, 0:2], chalo[:, idx, :]), reads=[chalo], writes=[cb])
                K.op(act, lambda e, cb=cb, p=p: e.activation(out=cb[:, 2:NTOK + 2], in_=p[:, 0:NTOK], func=AF.Copy),
                     reads=[p], writes=[cb])
                K.op(act, lambda e, ac=ac, p=p, idx=idx: e.activation(out=ac[:, :], in_=p[:, 0:NTOK], func=AF.Identity,
                                                                    scale=cp[:, CP_CW2 + idx:CP_CW2 + idx + 1],
                                                                    bias=cp[:, CP_CB + idx:CP_CB + idx + 1]),
                     reads=[p, cp], writes=[ac])
                K.op(pool, lambda e, cb=cb, idx=idx: e.tensor_copy(chalo[:, idx, :], cb[:, NTOK:NTOK + 2]), reads=[cb], writes=[chalo])
                K.op(dve, lambda e, ac=ac, cb=cb, idx=idx: e.scalar_tensor_tensor(ac[:, :], cb[:, 1:NTOK + 1], cp[:, CP_CW1 + idx:CP_CW1 + idx + 1],
                                                                              ac[:, :], ALU.mult, ALU.add),
                     reads=[cb, cp, ac], writes=[ac])
                K.op(dve, lambda e, ac=ac, cb=cb, idx=idx: e.scalar_tensor_tensor(ac[:, :], cb[:, 0:NTOK], cp[:, CP_CW0 + idx:CP_CW0 + idx + 1],
                                                                              ac[:, :], ALU.mult, ALU.add),
                     reads=[cb, cp, ac], writes=[ac])
                if q < 2:
                    gs = gsil[q]
                    K.op(act, lambda e, gs=gs, ac=ac: e.activation(out=gs[:, :], in_=ac[:, :], func=AF.Silu), reads=[ac], writes=[gs])
                else:
                    gs = gsil[q - 2]
                    K.op(pool, lambda e, gs=gs, ac=ac, t=t, q=q: e.tensor_tensor(hidden[:, 2 * t + q - 2, :], ac[:, :], gs[:, :], ALU.mult),
                         reads=[ac, gs], writes=[hidden])
        for half in range(2):
            pss = [K.ps() for _ in range(NSUB)]
            kcs = [(0, 8), (8, 16), (16, 22)]
            for ti, (k0, k1) in enumerate(kcs):
                wb = wload(l, T_DN + half * 3 + ti)
                w3 = wb[:, :].rearrange("p (k c) -> p k c", k=8)
                for kc in range(k0, k1):
                    for s in range(NSUB):
                        K.op(pe, lambda e, s=s, kc=kc, k0=k0, w3=w3, pss=pss: e.matmul(pss[s][:, :], hidden[:, kc, s * 128:(s + 1) * 128],
                                                                                   w3[:, kc - k0, :], start=(kc == 0), stop=(kc == 21)),
                             reads=[hidden, wb], writes=[pss[s]], signal=(kc == 21))
            for s in range(NSUB):
                K.op(dve, lambda e, s=s, half=half, pss=pss: e.tensor_tensor(xt[:, s, half * 512:(half + 1) * 512], pss[s][:, :],
                                                                           xt[:, s, half * 512:(half + 1) * 512], ALU.add),
                     reads=[pss[s], xt], writes=[xt])
        dst = y_out if last else xmid
        if last:
            K.op(pool, lambda e: e.memset(ssq[:, 0:NSUB], 0.0), writes=[ssq])
            for s in range(NSUB):
                K.op(act, lambda e, s=s: e.activation(out=xn[:, s, :], in_=xt[:, s, :], func=AF.Square, accum_out=ssq[:, s:s + 1]),
                     reads=[xt], writes=[xn, ssq])
            rstd_from(ssq, NSUB, 1.0 / D, EPS)
            for s in range(NSUB):
                K.op(dve, lambda e, s=s: e.scalar_tensor_tensor(xt[:, s, :], xt[:, s, :], ssq[:, 4 + s:5 + s], fing[:, :], ALU.mult, ALU.mult),
                     reads=[xt, ssq, fing], writes=[xt])
        K.dma(sp, dst[st * NTOK:(st + 1) * NTOK, :].rearrange("(s p) d -> p s d", p=128), xt[:, :, :], reads=[xt], writes=[dst],
              key="st_" + dst.name)

    def merge(l):
        first = True
        for g in range(3):
            if ("rw", "ret", "sgu")[g] not in stages:
                continue
            wbrg = wload(l, T_BR + g)
            wb3 = wbrg[:, :].rearrange("p (k c) -> p k c", k=4)
            for half in range(2):
                wgt = wload(l, T_GATE + 2 * g + half)
                wg = wgt[:, :].rearrange("p (k c) -> p k c", k=8)
                for dcl in range(4):
                    dc = half * 4 + dcl
                    pg = K.ps()
                    for kc in range(8):
                        mm(pg[:, 0:NTOK], wg[:, kc, dcl * 128:(dcl + 1) * 128], hT[:, kc, :], [wgt, hT], [pg], start=(kc == 0), stop=(kc == 7), signal=(kc == 7))
                    gt = gts[dc % 3]
                    K.op(act, lambda e, gt=gt, pg=pg: e.activation(out=gt[:, :], in_=pg[:, 0:NTOK], func=AF.Sigmoid), reads=[pg], writes=[gt])
                    pp = K.ps()
                    for kc in range(4):
                        mm(pp[:, 0:NTOK], wb3[:, kc, dc * 128:(dc + 1) * 128], ybr[g][:, kc, :], [wbrg, ybr[g]], [pp], start=(kc == 0), stop=(kc == 3), signal=(kc == 3))
                    if first:
                        K.op(dve, lambda e, pp=pp, gt=gt, dc=dc: e.tensor_tensor(mrg[:, dc, :], pp[:, 0:NTOK], gt[:, :], ALU.mult), reads=[pp, gt], writes=[mrg])
                    else:
                        tmp = macc[dc % 2]
                        K.op(dve, lambda e, pp=pp, gt=gt, tmp=tmp: e.tensor_tensor(tmp[:, :], pp[:, 0:NTOK], gt[:, :], ALU.mult), reads=[pp, gt], writes=[tmp])
                        K.op(pool, lambda e, tmp=tmp, dc=dc: e.tensor_tensor(mrg[:, dc, :], mrg[:, dc, :], tmp[:, :], ALU.add), reads=[tmp, mrg], writes=[mrg])
            first = False
        for half in range(2):
            wb = wload(l, T_OUT + half)
            w3 = wb[:, :].rearrange("p (k c) -> p k c", k=8)
            for s in range(NSUB):
                p = K.ps()
                for kc in range(8):
                    K.op(pe, lambda e, kc=kc, s=s, p=p, w3=w3: e.matmul(p[:, :], mrg[:, kc, s * 128:(s + 1) * 128], w3[:, kc, :],
                                                                    start=(kc == 0), stop=(kc == 7)),
                         reads=[mrg, wb], writes=[p], signal=(kc == 7))
                K.op(dve, lambda e, s=s, half=half, p=p: e.tensor_tensor(xt[:, s, half * 512:(half + 1) * 512], p[:, :],
                                                                       xt[:, s, half * 512:(half + 1) * 512], ALU.add),
                     reads=[p, xt], writes=[xt])

    C_ID, C_ONES, C_BLK, C_TRI, C_DECT, C_QDEC, C_KDEC, C_CDEC, C_RST, C_MG, C_MGT, C_I8 = 0, 128, 256, 384, 512, 1024, 1280, 1536, 1600, 1856, 2368, 2880

    def flat(ap):
        return ap.rearrange("p a b -> p (a b)")

    def mm(out, lhsT, rhs, reads, writes, start=True, stop=True, signal=False):
        K.op(pe, lambda e: e.matmul(out, lhsT, rhs, start=start, stop=stop), reads=reads, writes=writes, signal=signal)

    def rsqrt_inplace(b, ap_):
        K.op(act, lambda e: e.activation(out=ap_, in_=ap_, func=AF.Sqrt), reads=[b], writes=[b])
        K.op(dve, lambda e: e.reciprocal(ap_, ap_), reads=[b], writes=[b])

    def layer_setup(l):
        cp = colp[l]
        if l == 0:
            K.op(dve, lambda e: e.tensor_copy(onesb[:, :], cst[:, C_ONES:C_ONES + 128]), reads=[cst], writes=[onesb])
            K.op(dve, lambda e: e.tensor_copy(blkb[:, :], cst[:, C_BLK:C_BLK + 128]), reads=[cst], writes=[blkb])
            K.op(pool, lambda e: e.memset(VV.a.rearrange("p a b c d -> p (a b c d)"), 0.0), writes=[VV])
        K.dma(sp, rowp[:, :], rowp_d[l, :, :], writes=[rowp])
        K.dma(sp, lowr32[:, :], lowr_d[l, :, :], writes=[lowr32])
        K.dma(sp, sgw32[:, :], sgw_d[l, :, :], writes=[sgw32])
        K.op(dve, lambda e: e.tensor_copy(lowr[:, :], lowr32[:, :]), reads=[lowr32], writes=[lowr])
        for g in range(4):
            K.op(dve, lambda e, g=g: e.tensor_tensor(wsT[:, g, :], sgw32[:, g * 128:(g + 1) * 128], cst[:, C_TRI:C_TRI + 128], ALU.mult),
                 reads=[sgw32, cst], writes=[wsT])
        K.op(dve, lambda e: e.tensor_scalar(omka[:, :], cp[:, CP_KA:CP_KA + 4], -1.0, 1.0, ALU.mult, ALU.add), reads=[cp], writes=[omka])
        for b_ in (R32, Rb, ST32, STb, phalo):
            K.op(pool, lambda e, b_=b_: e.memset(b_.a, 0.0), writes=[b_])

    def proj_fm(wb, q, evac):
        w3 = wb[:, :].rearrange("p (k c) -> p k c", k=8)
        p = K.ps()
        for kc in range(8):
            mm(p[:, 0:NTOK], w3[:, kc, q * 128:(q + 1) * 128], hT[:, kc, :], [wb, hT], [p], start=(kc == 0), stop=(kc == 7), signal=(kc == 7))
        evac(p)

    def proj_tm(wb, s, evac):
        w3 = wb[:, :].rearrange("p (k c) -> p k c", k=8)
        p = K.ps()
        for kc in range(8):
            mm(p[:, :], hT[:, kc, s * 128:(s + 1) * 128], w3[:, kc, :], [wb, hT], [p], start=(kc == 0), stop=(kc == 7), signal=(kc == 7))
        evac(p)

    def sgu(l, st):
        wb = wload(l, T_SGU)
        for cc in range(4):
            proj_fm(wb, cc, lambda p, cc=cc: K.op(act, lambda e: e.activation(out=uT[:, cc, :], in_=p[:, 0:NTOK], func=AF.Gelu), reads=[p], writes=[uT]))
        wb = wload(l, T_SGV)
        for s in range(NSUB):
            v = vg[s % 2]
            vb = vnb[s % 2]
            K.op(pool, lambda e: e.memset(st8[:, 0:2], 0.0), writes=[st8])
            proj_tm(wb, s, lambda p, v=v: K.op(act, lambda e: e.activation(out=v[:, :], in_=p[:, :], func=AF.Gelu), reads=[p], writes=[v]))
            K.op(act, lambda e, v=v: e.activation(out=t512[0][:, :], in_=v[:, :], func=AF.Copy, accum_out=st8[:, 0:1]), reads=[v], writes=[t512[0], st8])
            K.op(act, lambda e, v=v: e.activation(out=t512[0][:, :], in_=v[:, :], func=AF.Square, accum_out=st8[:, 1:2]), reads=[v], writes=[t512[0], st8])
            K.op(dve, lambda e: e.tensor_scalar(st8[:, 2:4], st8[:, 0:2], 1.0 / 512, None, ALU.mult), reads=[st8], writes=[st8])
            K.op(dve, lambda e: e.tensor_tensor(st8[:, 4:5], st8[:, 2:3], st8[:, 2:3], ALU.mult), reads=[st8], writes=[st8])
            K.op(dve, lambda e: e.tensor_tensor(st8[:, 5:6], st8[:, 3:4], st8[:, 4:5], ALU.subtract), reads=[st8], writes=[st8])
            K.op(dve, lambda e: e.tensor_scalar(st8[:, 6:7], st8[:, 5:6], EPS, None, ALU.add), reads=[st8], writes=[st8])
            rsqrt_inplace(st8, st8[:, 6:7])
            K.op(dve, lambda e, v=v: e.tensor_scalar(t512[1][:, :], v[:, :], st8[:, 2:3], st8[:, 6:7], ALU.subtract, ALU.mult), reads=[v, st8], writes=[t512[1]])
            K.op(dve, lambda e: e.tensor_tensor(t512[1][:, :], t512[1][:, :], rowp[:, 0:512], ALU.mult), reads=[t512[1], rowp], writes=[t512[1]])
            K.op(pool, lambda e, vb=vb: e.tensor_tensor(vb[:, :], t512[1][:, :], rowp[:, 512:1024], ALU.add), reads=[t512[1], rowp], writes=[vb])
            pm = K.ps()
            for g in range(4):
                mm(pm[:, g * 128:(g + 1) * 128], vb[:, g * 128:(g + 1) * 128], wsT[:, g, :], [vb, wsT], [pm], signal=(g == 3))
            K.op(dve, lambda e, pm=pm: e.tensor_tensor(t512[2][:, :], pm[:, :], rowp[:, 1024:1536], ALU.add), reads=[pm, rowp], writes=[t512[2]])
            K.op(pool, lambda e, s=s: e.tensor_tensor(ybr[2][:, :, s * 128:(s + 1) * 128], t512[2][:, :].rearrange("p (g i) -> p g i", g=4),
                                                      uT[:, :, s * 128:(s + 1) * 128], ALU.mult), reads=[t512[2], uT], writes=[ybr[2]])

    def ret(l, st):
        wb = wload(l, T_RETQK)
        for s in range(NSUB):
            def ev(p):
                K.op(act, lambda e: e.activation(out=qkraw[:, 0:256], in_=p[:, 0:256], func=AF.Copy), reads=[p], writes=[qkraw])
                K.op(act, lambda e: e.activation(out=qkraw[:, 256:512], in_=p[:, 256:512], func=AF.Copy, scale=0.125), reads=[p], writes=[qkraw])
            SUB = int(os.environ.get("RETSUB", "9"))
            proj_tm(wb, s, ev)
            if SUB < 2:
                continue
            q4 = qkraw[:, :].rearrange("p (h a f) -> p h a f", h=8, a=2)
            o4 = qkr[:, :].rearrange("p (h a f) -> p h a f", h=8, a=2)
            K.dma(sp, rot[:, :], rot_d[st * NTOK + s * 128:st * NTOK + (s + 1) * 128, :], writes=[rot])
            cosb = rot[:, 0:256].rearrange("p (h f) -> p h f", h=8)
            sinb = rot[:, 256:512].rearrange("p (h f) -> p h f", h=8)
            r3 = [t[:, :].rearrange("p (h f) -> p h f", h=8) for t in rtmp]
            K.op(dve, lambda e: e.tensor_tensor(r3[0], q4[:, :, 0, :], cosb, ALU.mult), reads=[qkraw, rot], writes=[rtmp[0]])
            K.op(pool, lambda e: e.tensor_tensor(r3[1], q4[:, :, 1, :], sinb, ALU.mult), reads=[qkraw, rot], writes=[rtmp[1]])
            K.op(dve, lambda e: e.tensor_tensor(o4[:, :, 0, :], r3[0], r3[1], ALU.subtract), reads=[rtmp[0], rtmp[1]], writes=[qkr])
            K.op(dve, lambda e: e.tensor_tensor(r3[2], q4[:, :, 1, :], cosb, ALU.mult), reads=[qkraw, rot], writes=[rtmp[2]])
            K.op(pool, lambda e: e.tensor_tensor(r3[3], q4[:, :, 0, :], sinb, ALU.mult), reads=[qkraw, rot], writes=[rtmp[3]])
            K.op(pool, lambda e: e.tensor_tensor(o4[:, :, 1, :], r3[2], r3[3], ALU.add), reads=[rtmp[2], rtmp[3]], writes=[qkr])
            if SUB < 3:
                continue
            K.op(dve, lambda e, s=s: e.tensor_tensor(kd[:, s, :], qkr[:, 256:512], cst[:, C_KDEC:C_KDEC + 256], ALU.mult), reads=[qkr, cst], writes=[kd])
            if SUB < 4:
                continue
            pt = K.ps()
            ptv = pt[:, :].bitcast(BF16)
            for j in range(4):
                K.op(pe, lambda e, j=j: e.transpose(ptv[:, j * 128:(j + 1) * 128], qkr[:, j * 128:(j + 1) * 128], ident[:, :]),
                     reads=[qkr, ident], writes=[pt], signal=(j == 3))
            if SUB < 5:
                continue
            K.op(dve, lambda e, s=s: e.tensor_copy(flat(qkT[:, s, :, :]), ptv[:, 0:512]), reads=[pt], writes=[qkT])
            K.op(dve, lambda e, s=s: e.tensor_tensor(flat(qdT[:, s, :, :]), flat(qkT[:, s, 0:2, :]), cst[:, C_QDEC:C_QDEC + 256], ALU.mult), reads=[qkT, cst], writes=[qdT])
        LVL = int(os.environ.get("RETLVL", "9"))
        if LVL < 2:
            return
        wb = wload(l, T_RETV)
        for s in range(NSUB):
            proj_tm(wb, s, lambda p, s=s: K.op(act, lambda e: e.activation(out=vtm[:, s, :], in_=p[:, :], func=AF.Copy), reads=[p], writes=[vtm]))
        wb = wload(l, T_RETG)
        for cc in range(4):
            proj_fm(wb, cc, lambda p, cc=cc: K.op(act, lambda e: e.activation(out=gT[:, cc, :], in_=p[:, 0:NTOK], func=AF.Silu), reads=[p], writes=[gT]))
        if LVL < 3:
            return
        for s in range(NSUB):
            psc = [K.ps(), K.ps()]
            for h in range(4):
                hp, par = h // 2, h % 2
                pb = par * 64
                mm(psc[par][:, hp * 128:(hp + 1) * 128], qkT[pb:pb + 64, s, 2 + hp, :], qkT[pb:pb + 64, s, hp, :], [qkT], [psc[par]], signal=(hp == 1))
            sm4 = smT.a.rearrange("p (hp par) i -> p hp par i", par=2)
            dec4 = cst[:, C_DECT:C_DECT + 512].rearrange("p (hp par i) -> p hp par i", hp=2, par=2)
            for par in range(2):
                K.op(dve, lambda e, par=par: e.tensor_tensor(sm4[:, :, par, :], psc[par][:, 0:256].rearrange("p (hp i) -> p hp i", hp=2), dec4[:, :, par, :], ALU.mult),
                     reads=[psc[par], cst], writes=[smT])
            if LVL < 4:
                continue
            py = K.ps()
            for h in range(4):
                hp, pb = h // 2, (h % 2) * 64
                mm(py[:, h * 128:(h + 1) * 128], vtm[:, s, h * 128:(h + 1) * 128], smT[:, h, :], [vtm, smT], [py], start=True, stop=False)
                mm(py[:, h * 128:(h + 1) * 128], Rb[:, h, :], qdT[:, s, hp, :], [Rb, qdT], [py], start=False, stop=True, signal=(h == 3))
            K.op(act, lambda e: e.activation(out=sqb[:, :], in_=py[:, :], func=AF.Square), reads=[py], writes=[sqb])
            pss = K.ps()
            mm(pss[:, :], onesb[:, :], sqb[:, :], [onesb, sqb], [pss], signal=True)
            K.op(dve, lambda e: e.tensor_scalar(t512[0][:, :], pss[:, :], 1.0 / 128, EPS, ALU.mult, ALU.add), reads=[pss], writes=[t512[0]])
            rsqrt_inplace(t512[0], t512[0][:, :])
            K.op(dve, lambda e: e.tensor_tensor(t512[1][:, :], py[:, :], t512[0][:, :], ALU.mult), reads=[py, t512[0]], writes=[t512[1]])
            K.op(pool, lambda e, s=s: e.tensor_tensor(ybr[1][:, :, s * 128:(s + 1) * 128], t512[1][:, :].rearrange("p (g i) -> p g i", g=4),
                                                      gT[:, :, s * 128:(s + 1) * 128], ALU.mult), reads=[t512[1], gT], writes=[ybr[1]])
            if LVL < 5:
                continue
            pkv = K.ps()
            for h in range(4):
                hp, pb = h // 2, (h % 2) * 64
                mm(pkv[pb:pb + 64, hp * 128:(hp + 1) * 128], kd[:, s, h * 64:(h + 1) * 64], vtm[:, s, h * 128:(h + 1) * 128], [kd, vtm], [pkv], signal=(h == 3))
            for hp in range(2):
                K.op(dve, lambda e, hp=hp: e.tensor_scalar(R32[:, hp, :], R32[:, hp, :], cst[:, C_CDEC + hp:C_CDEC + hp + 1], None, ALU.mult),
                     reads=[R32, cst], writes=[R32])
            K.op(dve, lambda e: e.tensor_tensor(flat(R32[:, :, :]), flat(R32[:, :, :]), pkv[:, 0:256], ALU.add), reads=[R32, pkv], writes=[R32])
            for h in range(4):
                hp, pb = h // 2, (h % 2) * 64
                K.op(act, lambda e, h=h, hp=hp, pb=pb: e.activation(out=Rb[pb:pb + 64, h, :], in_=R32[pb:pb + 64, hp, :], func=AF.Copy), reads=[R32], writes=[Rb])

    def c3(ap):
        return ap.rearrange("p (c t) -> p c t", t=64)

    def rwkv(l, st):
        cp = colp[l]
        blk32 = cst[:, C_BLK:C_BLK + 128]
        wb = None
        for c in range(14):
            if c % 4 == 0:
                wb = wload(l, T_RW + c // 4)
            pr = praw[c % 2]
            K.op(pool, lambda e, c=c, pr=pr: e.tensor_copy(pr[:, 0:1], phalo[:, c:c + 1]), reads=[phalo], writes=[pr])
            proj_fm(wb, c % 4, lambda p, pr=pr: K.op(act, lambda e: e.activation(out=pr[:, 1:NTOK + 1], in_=p[:, 0:NTOK], func=AF.Copy), reads=[p], writes=[pr]))
            K.op(pool, lambda e, c=c, pr=pr: e.tensor_copy(phalo[:, c:c + 1], pr[:, NTOK:NTOK + 1]), reads=[pr], writes=[phalo])
            dtmp = rt[c % 2]
            K.op(dve, lambda e, pr=pr, dtmp=dtmp: e.tensor_tensor(dtmp[:, :], pr[:, 0:NTOK], pr[:, 1:NTOK + 1], ALU.subtract), reads=[pr], writes=[dtmp])
            if c < 12:
                dstb = (rF, kF, vF)[c // 4]
                dst = dstb[:, c % 4, :]
            else:
                dstb = rt[2 + (c % 2)]
                dst = dstb[:, :]
            K.op(dve, lambda e, pr=pr, dtmp=dtmp, dst=dst, c=c: e.scalar_tensor_tensor(dst, dtmp[:, :], cp[:, CP_MU + c:CP_MU + c + 1], pr[:, 1:NTOK + 1], ALU.mult, ALU.add),
                 reads=[dtmp, pr, cp], writes=[dstb])
            if c == 12:
                K.op(act, lambda e, dstb=dstb: e.activation(out=lrb[0:64, 0, :], in_=dstb[0:64, :], func=AF.Tanh), reads=[dstb], writes=[lrb])
                K.op(act, lambda e, dstb=dstb: e.activation(out=lrb[64:128, 0, :], in_=dstb[64:128, :], func=AF.Copy), reads=[dstb], writes=[lrb])
            if c == 13:
                K.op(act, lambda e, dstb=dstb: e.activation(out=lrb[:, 1, :], in_=dstb[:, :], func=AF.Sigmoid), reads=[dstb], writes=[lrb])
        for cc in range(4):
            pw, pa, pg = K.ps(), K.ps(), K.ps()
            mm(pw[:, 0:NTOK], lowr[0:64, cc * 128:(cc + 1) * 128], lrb[0:64, 0, :], [lowr, lrb], [pw], signal=True)
            mm(pa[:, 0:NTOK], lowr[64:128, cc * 128:(cc + 1) * 128], lrb[64:128, 0, :], [lowr, lrb], [pa], signal=True)
            mm(pg[:, 0:NTOK], lowr[:, 512 + cc * 128:512 + (cc + 1) * 128], lrb[:, 1, :], [lowr, lrb], [pg], signal=True)
            sg, a_, cum, ginv, gprev, kk, sq, t1, b_, prod = rt
            K.op(act, lambda e: e.activation(out=sg[:, :], in_=pw[:, 0:NTOK], func=AF.Sigmoid, bias=cp[:, CP_W0 + cc:CP_W0 + cc + 1]), reads=[pw, cp], writes=[sg])
            K.op(act, lambda e: e.activation(out=a_[:, :], in_=pa[:, 0:NTOK], func=AF.Sigmoid, bias=cp[:, CP_A0 + cc:CP_A0 + cc + 1]), reads=[pa, cp], writes=[a_])
            K.op(act, lambda e: e.activation(out=gF[:, cc, :], in_=pg[:, 0:NTOK], func=AF.Copy), reads=[pg], writes=[gF])
            K.op(dve, lambda e: e.tensor_tensor_scan(cum[:, :], cst[:, C_RST:C_RST + NTOK], sg[:, :], 0.0, ALU.mult, ALU.add), reads=[cst, sg], writes=[cum])
            K.op(act, lambda e: e.activation(out=gam[:, cc, :], in_=cum[:, :], func=AF.Exp, scale=-C0), reads=[cum], writes=[gam])
            K.op(act, lambda e: e.activation(out=ginv[:, :], in_=cum[:, :], func=AF.Exp, scale=C0), reads=[cum], writes=[ginv])
            K.op(dve, lambda e: e.tensor_tensor(gprev[:, :], cum[:, :], sg[:, :], ALU.subtract), reads=[cum, sg], writes=[gprev])
            K.op(act, lambda e: e.activation(out=gprev[:, :], in_=gprev[:, :], func=AF.Exp, scale=-C0), reads=[gprev], writes=[gprev])
            K.op(dve, lambda e: e.tensor_scalar(kk[:, :], kF[:, cc, :], cp[:, CP_KK + cc:CP_KK + cc + 1], None, ALU.mult), reads=[kF, cp], writes=[kk])
            K.op(pool, lambda e: e.tensor_tensor(sq[:, :], kk[:, :], kk[:, :], ALU.mult), reads=[kk], writes=[sq])
            pss = K.ps()
            mm(pss[:, 0:NTOK], blk32, sq[:, :], [cst, sq], [pss], signal=True)
            K.op(dve, lambda e: e.tensor_scalar(sq[:, :], pss[:, 0:NTOK], 1e-24, None, ALU.max), reads=[pss], writes=[sq])
            rsqrt_inplace(sq, sq[:, :])
            K.op(dve, lambda e: e.tensor_tensor(kk[:, :], kk[:, :], sq[:, :], ALU.mult), reads=[kk, sq], writes=[kk])
            K.op(dve, lambda e: e.tensor_scalar(t1[:, :], a_[:, :], cp[:, CP_KA + cc:CP_KA + cc + 1], omka[:, cc:cc + 1], ALU.mult, ALU.add), reads=[a_, cp, omka], writes=[t1])
            K.op(pool, lambda e: e.tensor_tensor(t1[:, :], kF[:, cc, :], t1[:, :], ALU.mult), reads=[kF, t1], writes=[t1])
            K.op(dve, lambda e: e.tensor_tensor(b_[:, :], kk[:, :], a_[:, :], ALU.mult), reads=[kk, a_], writes=[b_])
            K.op(dve, lambda e: e.tensor_tensor(AR[:, cc, :, 1, :], c3(rF[:, cc, :]), c3(gam[:, cc, :]), ALU.mult), reads=[rF, gam], writes=[AR])
            K.op(dve, lambda e: e.scalar_tensor_tensor(AR[:, cc, :, 0, :], c3(kk[:, :]), -1.0, c3(gprev[:, :]), ALU.mult, ALU.mult), reads=[kk, gprev], writes=[AR])
            K.op(pool, lambda e: e.tensor_tensor(BK[:, cc, :, 0, :], c3(b_[:, :]), c3(ginv[:, :]), ALU.mult), reads=[b_, ginv], writes=[BK])
            K.op(pool, lambda e: e.tensor_tensor(BK[:, cc, :, 1, :], c3(t1[:, :]), c3(ginv[:, :]), ALU.mult), reads=[t1, ginv], writes=[BK])
            K.op(act, lambda e: e.activation(out=VV[:, cc, :, 1, :], in_=c3(vF[:, cc, :]), func=AF.Copy), reads=[vF], writes=[VV])
            K.op(dve, lambda e: e.scalar_tensor_tensor(prod[:, :], rF[:, cc, :], cp[:, CP_RK + cc:CP_RK + cc + 1], t1[:, :], ALU.mult, ALU.mult), reads=[rF, cp, t1], writes=[prod])
            pbn = K.ps()
            mm(pbn[:, 0:NTOK], blk32, prod[:, :], [cst, prod], [pbn], signal=True)
            K.op(dve, lambda e: e.tensor_tensor(bonus[:, cc, :], pbn[:, 0:NTOK], vF[:, cc, :], ALU.mult), reads=[pbn, vF], writes=[bonus])
        for c in range(NCH):
            pg1 = [K.ps(), K.ps()]
            for h in range(8):
                cc, par = h // 2, h % 2
                pb = par * 64
                mm(pg1[par][:, cc * 128:(cc + 1) * 128], flat(BK[pb:pb + 64, cc, c, :, :]), flat(AR[pb:pb + 64, cc, c, :, :]), [BK, AR], [pg1[par]], signal=(cc == 3))
            G5 = G.a.rearrange("p c (a par) m -> p c a par m", par=2)
            mg3 = cst[:, C_MG:C_MG + 512].rearrange("p (a m) -> p a m", a=4)
            for par in range(2):
                K.op(dve, lambda e, par=par: e.tensor_tensor(G5[:, c, :, par, :], pg1[par][:, :].rearrange("p (a m) -> p a m", a=4), mg3, ALU.mult), reads=[pg1[par], cst], writes=[G])
            pg2 = [K.ps(), K.ps()]
            for h in range(8):
                cc, par = h // 2, h % 2
                pb = par * 64
                mm(pg2[par][0:64, cc * 64:(cc + 1) * 64], AR[pb:pb + 64, cc, c, 0, :], BK[pb:pb + 64, cc, c, 0, :], [AR, BK], [pg2[par]], signal=(cc == 3))
            NT5 = NTn[0].a.rearrange("p c (a par) m -> p c a par m", par=2)
            mgt3 = cst[0:64, C_MGT:C_MGT + 256].rearrange("p (a m) -> p a m", a=4)
            for par in range(2):
                K.op(dve, lambda e, par=par: e.tensor_tensor(NT5[:, c, :, par, :], pg2[par][0:64, 0:256].rearrange("p (a m) -> p a m", a=4), mgt3, ALU.mult), reads=[pg2[par], cst], writes=[NTn[0]])
            K.op(dve, lambda e: e.tensor_tensor(Pp[0][:, c, :, :], G[0:64, c, :, 0:64], cst[0:64, C_I8:C_I8 + 512].rearrange("p (h s) -> p h s", h=8), ALU.add), reads=[G, cst], writes=[Pp[0]])
            for srcb, dstb in ((BK, BKT), (VV, VT)):
                pt = K.ps()
                ptv = pt[:, :].bitcast(BF16)
                for cc in range(4):
                    K.op(pe, lambda e, cc=cc, srcb=srcb: e.transpose(ptv[:, cc * 128:(cc + 1) * 128], flat(srcb[:, cc, c, :, :]), ident[:, :]), reads=[srcb, ident], writes=[pt], signal=(cc == 3))
                K.op(dve, lambda e, dstb=dstb: e.tensor_copy(flat(dstb[:, c, :, :]), ptv[:, 0:512]), reads=[pt], writes=[dstb])
        for lev in range(1, 6):
            for c in range(NCH):
                def Nprev(h):
                    return G[0:64, c, h, 0:64] if lev == 1 else Nn[(lev - 1) % 2][:, c, h, :]
                nprev_b = G if lev == 1 else Nn[(lev - 1) % 2]
                ntprev_b = NTn[(lev - 1) % 2]
                if lev < 5:
                    pn = K.ps()
                    for h in range(8):
                        mm(pn[0:64, h * 64:(h + 1) * 64], ntprev_b[:, c, h, :], Nprev(h), [ntprev_b, nprev_b], [pn], signal=(h == 7))
                    K.op(act, lambda e, pn=pn: e.activation(out=flat(Nn[lev % 2][:, c, :, :]), in_=pn[0:64, :], func=AF.Copy), reads=[pn], writes=[Nn[lev % 2]])
                pnt = K.ps()
                for h in range(8):
                    mm(pnt[0:64, h * 64:(h + 1) * 64], Nprev(h), ntprev_b[:, c, h, :], [ntprev_b, nprev_b], [pnt], signal=(h == 7))
                K.op(dve, lambda e, pnt=pnt: e.tensor_copy(flat(NTn[lev % 2][:, c, :, :]), pnt[0:64, :]), reads=[pnt], writes=[NTn[lev % 2]])
                pp = K.ps()
                for h in range(8):
                    mm(pp[0:64, h * 64:(h + 1) * 64], NTn[lev % 2][:, c, h, :], Pp[(lev - 1) % 2][:, c, h, :], [NTn[lev % 2], Pp[(lev - 1) % 2]], [pp], signal=(h == 7))
                K.op(dve, lambda e, pp=pp: e.tensor_tensor(flat(Pp[lev % 2][:, c, :, :]), pp[0:64, :], flat(Pp[(lev - 1) % 2][:, c, :, :]), ALU.add), reads=[pp, Pp[(lev - 1) % 2]], writes=[Pp[lev % 2]])
        Tm = Pp[1]
        for c in range(NCH):
            px = K.ps()
            for h in range(8):
                cc, pb = h // 2, (h % 2) * 64
                mm(px[0:64, h * 64:(h + 1) * 64], AR[:, cc, c, 0, :], STb[:, h, :], [AR, STb], [px], start=True, stop=False)
                mm(px[0:64, h * 64:(h + 1) * 64], G[:, c, h, 0:64], VT[:, c, cc, pb:pb + 64], [G, VT], [px], start=False, stop=True, signal=(h == 7))
            K.op(act, lambda e: e.activation(out=flat(XT[:, :, :]), in_=px[0:64, :], func=AF.Copy), reads=[px], writes=[XT])
            pu = K.ps()
            for h in range(8):
                mm(pu[0:64, h * 64:(h + 1) * 64], Tm[:, c, h, :], XT[:, h, :], [Tm, XT], [pu], signal=(h == 7))
            K.op(dve, lambda e: e.tensor_copy(flat(VT[0:64, c, :, :]), pu[0:64, :]), reads=[pu], writes=[VT])
            py = K.ps()
            pst = K.ps()
            for h in range(8):
                cc, pb = h // 2, (h % 2) * 64
                o = py[pb:pb + 64, cc * 64:(cc + 1) * 64]
                mm(o, STb[:, h, :], AR[:, cc, c, 1, :], [STb, AR], [py], start=True, stop=False)
                mm(o, VT[:, c, cc, pb:pb + 64], G[:, c, h, 64:128], [VT, G], [py], start=False, stop=True, signal=(h == 7))
            for h in range(8):
                cc, pb = h // 2, (h % 2) * 64
                mm(pst[pb:pb + 64, cc * 64:(cc + 1) * 64], BKT[:, c, cc, pb:pb + 64], VT[:, c, cc, pb:pb + 64], [BKT, VT], [pst], signal=(h == 7))
            K.op(act, lambda e: e.activation(out=Yf[:, :, c * 64:(c + 1) * 64], in_=py[:, 0:256].rearrange("p (a t) -> p a t", a=4), func=AF.Copy), reads=[py], writes=[Yf])
            K.op(dve, lambda e: e.tensor_tensor(flat(sttmp[:, :, :]), pst[:, 0:256], flat(ST32[:, :, :]), ALU.add), reads=[pst, ST32], writes=[sttmp])
            for cc in range(4):
                K.op(dve, lambda e, cc=cc: e.tensor_scalar(ST32[:, cc, :], sttmp[:, cc, :], gam[:, cc, c * 64 + 63:c * 64 + 64], None, ALU.mult), reads=[sttmp, gam], writes=[ST32])
            st4 = STb.a.rearrange("p (c two) v -> p c two v", two=2)
            for par in range(2):
                K.op(act, lambda e, par=par: e.activation(out=st4[par * 64:(par + 1) * 64, :, par, :], in_=ST32[par * 64:(par + 1) * 64, :, :], func=AF.Copy), reads=[ST32], writes=[STb])
        for cc in range(4):
            yc, sq2, rs = rt[0], rt[1], rt[2]
            pm = K.ps()
            mm(pm[:, 0:NTOK], blk32, Yf[:, cc, :], [cst, Yf], [pm], signal=True)
            K.op(dve, lambda e: e.scalar_tensor_tensor(yc[:, :], pm[:, 0:NTOK], -1.0 / 64, Yf[:, cc, :], ALU.mult, ALU.add), reads=[pm, Yf], writes=[yc])
            K.op(pool, lambda e: e.tensor_tensor(sq2[:, :], yc[:, :], yc[:, :], ALU.mult), reads=[yc], writes=[sq2])
            pv = K.ps()
            mm(pv[:, 0:NTOK], blk32, sq2[:, :], [cst, sq2], [pv], signal=True)
            K.op(dve, lambda e: e.tensor_scalar(rs[:, :], pv[:, 0:NTOK], 1.0 / 64, GN_EPS, ALU.mult, ALU.add), reads=[pv], writes=[rs])
            rsqrt_inplace(rs, rs[:, :])
            K.op(dve, lambda e: e.tensor_tensor(yc[:, :], yc[:, :], rs[:, :], ALU.mult), reads=[yc, rs], writes=[yc])
            K.op(dve, lambda e: e.tensor_scalar(yc[:, :], yc[:, :], cp[:, CP_LG + cc:CP_LG + cc + 1], cp[:, CP_LB + cc:CP_LB + cc + 1], ALU.mult, ALU.add), reads=[yc, cp], writes=[yc])
            K.op(pool, lambda e: e.tensor_tensor(yc[:, :], yc[:, :], bonus[:, cc, :], ALU.add), reads=[yc, bonus], writes=[yc])
            K.op(pool, lambda e: e.tensor_tensor(ybr[0][:, cc, :], yc[:, :], gF[:, cc, :], ALU.mult), reads=[yc, gF], writes=[ybr[0]])

    gate_tiles = [None, None, None]

    for l in range(depth):
        last = (l == depth - 1)
        src = x_in if l == 0 else xmid
        K.op(pool, lambda e: e.memset(chalo[:, :, :], 0.0), writes=[chalo])
        if stages:
            layer_setup(l)
        for st in range(nst):
            K.dma(sp, xt[:, :, :], src[st * NTOK:(st + 1) * NTOK, :].rearrange("(s p) d -> p s d", p=128),
                  reads=[src], writes=[xt])
            if stages & {"rw", "ret", "sgu"}:
                rmsnorm_T(l, xt, CP_N1)
                if "sgu" in stages:
                    sgu(l, st)
                if "ret" in stages:
                    ret(l, st)
                if "rw" in stages:
                    rwkv(l, st)
                if DBG >= 0 and l == 0 and st == 0:
                    K.op(dve, lambda e: e.tensor_copy(t512[0][:, :], ybr[DBG].a.rearrange("p a t -> p (a t)")[:, 0:512]), reads=[ybr[DBG]], writes=[t512[0]])
                    K.op(dve, lambda e: e.tensor_copy(t512[1][:, :], ybr[DBG].a.rearrange("p a t -> p (a t)")[:, 512:1024]), reads=[ybr[DBG]], writes=[t512[1]])
                    K.dma(sp, dbg_d[:, 0:512], t512[0][:, :], reads=[t512[0]], writes=[dbg_d], key="dbg")
                    K.dma(sp, dbg_d[:, 512:1024], t512[1][:, :], reads=[t512[1]], writes=[dbg_d], key="dbg")
                merge(l)
            ffn(l, st, last)

    ds = K.dsem("st_y")
    nc.sync.wait_ge(ds[0], ds[1])
    if DBG >= 0:
        ds = K.dsem("dbg")
        nc.sync.wait_ge(ds[0], ds[1])
    return nc


def _arr_cols(w, cols):
    Kd = w.shape[0]
    cols = np.asarray(cols)
    wp = np.zeros((Kd, len(cols)), np.float32)
    ok = cols >= 0
    wp[:, ok] = w[:, cols[ok]]
    nt = len(cols) // 512
    return np.ascontiguousarray(wp.reshape(Kd // 128, 128, nt, 512).transpose(2, 1, 0, 3))


def _prep_layer_weights(w_in, w_branch, w_out, w_up, w_down):
    tiles = np.zeros((NWT, 128, 4096), np.float32)
    cols = list(range(0, 1792)) + [-1] * 256 + list(range(1792, 7424))
    tiles[0:15] = _arr_cols(w_in, cols).reshape(15, 128, 4096)
    for g in range(3):
        tiles[T_BR + g] = w_branch[g].reshape(4, 128, 1024).transpose(1, 0, 2).reshape(128, 4096)
    tiles[T_OUT:T_OUT + 2] = _arr_cols(w_out, list(range(1024))).reshape(2, 128, 4096)
    ucols = []
    for t in range(11):
        ucols += list(range(256 * t, 256 * t + 256)) + list(range(DFF + 256 * t, DFF + 256 * t + 256))
    tiles[T_UP:T_UP + 11] = _arr_cols(w_up, ucols).reshape(11, 128, 4096)
    wd = w_down.reshape(22, 128, 2, 512)
    for half in range(2):
        for ti, (k0, k1) in enumerate([(0, 8), (8, 16), (16, 22)]):
            blk = np.zeros((128, 8, 512), np.float32)
            blk[:, 0:k1 - k0, :] = wd[k0:k1, :, half, :].transpose(1, 0, 2)
            tiles[T_DN + half * 3 + ti] = blk.reshape(128, 4096)
    return tiles


def _pcol(v):
    return np.ascontiguousarray(np.asarray(v, np.float32).reshape(-1, 128).T)


def _host_prep(inp):
    L = DEPTH
    wbig = np.zeros((L, NWT, 128, 4096), np.float32)
    colp = np.zeros((L, 128, NCP), np.float32)
    rowp = np.zeros((L, 128, 1536), np.float32)
    lowr = np.zeros((L, 128, 1024), np.float32)
    sgw = np.zeros((L, 128, 512), np.float32)
    uidx = []
    for t in range(11):
        for q in range(4):
            base = 256 * t + 128 * q if q < 2 else DFF + 256 * t + 128 * (q - 2)
            uidx.append(base)
    for l in range(L):
        wbig[l] = _prep_layer_weights(inp["w_in"][l], inp["w_branch"][l], inp["w_out"][l], inp["ffn_w_up"][l], inp["ffn_w_down"][l])
        c = colp[l]
        c[:, CP_N1:CP_N1 + 8] = _pcol(inp["norm1_g"][l])
        c[:, CP_N2:CP_N2 + 8] = _pcol(inp["norm2_g"][l])
        c[:, CP_MU:CP_MU + 14] = _pcol(inp["rw_mu"][l])
        c[:, CP_W0:CP_W0 + 4] = _pcol(inp["rw_w0"][l])
        c[:, CP_A0:CP_A0 + 4] = _pcol(inp["rw_a0"][l])
        c[:, CP_KK:CP_KK + 4] = _pcol(inp["rw_k_k"][l])
        c[:, CP_KA:CP_KA + 4] = _pcol(inp["rw_k_a"][l])
        c[:, CP_RK:CP_RK + 4] = _pcol(inp["rw_r_k"][l].reshape(-1))
        c[:, CP_LG:CP_LG + 4] = _pcol(inp["rw_lnx_g"][l].reshape(-1))
        c[:, CP_LB:CP_LB + 4] = _pcol(inp["rw_lnx_b"][l].reshape(-1))
        for i, base in enumerate(uidx):
            for j, off in enumerate((CP_CW0, CP_CW1, CP_CW2)):
                c[:, off + i] = inp["ffn_conv_w"][l][j, base:base + 128]
            c[:, CP_CB + i] = inp["ffn_conv_b"][l][base:base + 128]
        rowp[l, :, 0:512] = inp["sg_ln_g"][l][None, :]
        rowp[l, :, 512:1024] = inp["sg_ln_b"][l][None, :]
        rowp[l, :, 1024:1536] = inp["sg_b"][l].reshape(1, 512)
        lowr[l, 0:64, 0:512] = inp["rw_w_up"][l]
        lowr[l, 64:128, 0:512] = inp["rw_a_up"][l]
        lowr[l, :, 512:1024] = inp["rw_g_up"][l]
        sgw[l] = inp["sg_w_s"][l].transpose(2, 0, 1).reshape(128, 512)
    fing = np.ascontiguousarray(np.broadcast_to(inp["final_g"][None, :], (128, D))).astype(np.float32)
    cst = np.zeros((128, 4096), np.float32)
    cst[:, 0:128] = np.eye(128, dtype=np.float32)
    cst[:, 128:256] = 1.0
    p_ = np.arange(128)
    cst[:, 256:384] = (p_[:, None] // 64 == p_[None, :] // 64)
    cst[:, 384:512] = (p_[:, None] <= p_[None, :])
    lg = np.log(1.0 - 2.0 ** (-5.0 - np.arange(4, dtype=np.float64)))
    dif = (p_[None, :] - p_[:, None]).astype(np.float64)
    for h in range(4):
        cst[:, 512 + h * 128:512 + (h + 1) * 128] = np.where(dif >= 0, np.exp(np.maximum(dif, 0) * lg[h]), 0.0)
    for hp in range(2):
        hh = 2 * hp + p_ // 64
        cst[:, 1024 + hp * 128:1024 + (hp + 1) * 128] = np.exp((p_[None, :] + 1.0) * lg[hh][:, None])
        cst[:, 1536 + hp] = np.exp(128.0 * lg[hh])
    for h in range(4):
        cst[:, 1280 + h * 64:1280 + (h + 1) * 64] = np.exp((127.0 - p_) * lg[h])[:, None]
    tt = np.arange(NTOK)
    cst[:, 1600:1600 + NTOK] = (tt % 64 != 0)[None, :]
    s_ = p_ % 64
    t64 = np.arange(64)
    mg = np.zeros((128, 128), np.float32)
    mg[:, 0:64] = (s_[:, None] < t64[None, :])
    mg[:, 64:128] = (s_[:, None] <= t64[None, :])
    for j in range(4):
        cst[:, 1856 + j * 128:1856 + (j + 1) * 128] = mg
    for h in range(8):
        cst[:, 2368 + h * 64:2368 + (h + 1) * 64] = (t64[None, :] < s_[:, None])
        cst[:, 2880 + h * 64:2880 + (h + 1) * 64] = (t64[None, :] == s_[:, None])
    inv_freq = (1.0 / (10000.0 ** np.linspace(0.0, 1.0, 32, dtype=np.float32))).astype(np.float32)
    ang = np.arange(SEQ, dtype=np.float32)[:, None] * inv_freq[None, :]
    rot = np.concatenate([np.tile(np.cos(ang), (1, 8)), np.tile(np.sin(ang), (1, 8))], axis=1).astype(np.float32)
    return dict(wbig=wbig, colp=colp, rowp=rowp, fing=fing, lowr=lowr, sgw=sgw, cst=cst, rot=rot)


_PROG_CACHE = {}


def run(inputs, nst=NST_FULL, depth=DEPTH, stages=("rw", "ret", "sgu"), ncores=8):
    stages = frozenset(stages)
    inp = {k: np.asarray(v, np.float32) for k, v in inputs.items()}
    shared = _host_prep(inp)
    key = (nst, depth, stages)
    if key not in _PROG_CACHE:
        _PROG_CACHE[key] = build_program(nst, depth, stages)
    nc = _PROG_CACHE[key]
    in_maps = []
    for c in range(ncores):
        m = dict(shared)
        m["x"] = np.ascontiguousarray(inp["x"][c % 4])
        in_maps.append(m)
    res = run_bass_kernel_spmd(nc, in_maps, core_ids=list(range(ncores)))
    if "dbg" in res.results[0]:
        global LAST_DBG
        LAST_DBG = res.results[0]["dbg"]
    return np.stack([res.results[c % ncores]["y"] for c in range(4)], axis=0)


def kernel(**inputs):
    return run(inputs).astype(np.float32)
```
